# Optimizing a Trainium2 kernel written in Bass

```python
import jax, jax.numpy as jnp
from jax import lax
import numpy as np


D_MODEL = 1024
BATCH = 8
SEQ = 4096
DEPTH = 2

HEAD_DIM = 64
FOX_HEADS = 4
DIL_HEADS = 4
DIL_PATTERNS = ((128, 1), (512, 4), (2048, 16))
RET_HEADS = 4
RET_CHUNK = 128
MLA_HEADS = 4
MLA_Q_RANK = 256
MLA_KV_RANK = 128
MLA_NOPE = 64
MLA_ROPE = 32
MLA_V = 64
ROPE_THETA = 10000.0
Q_BLOCK = 128
N_BRANCH = 4
BRANCH_WIDTH = 4 * HEAD_DIM
D_FF = ((8 * D_MODEL + 3 * 256 - 1) // (3 * 256)) * 256
RMS_EPS = 1e-6
GN_EPS = 1e-5

FOX_W = FOX_HEADS * HEAD_DIM
DIL_W = DIL_HEADS * HEAD_DIM
RET_W = RET_HEADS * HEAD_DIM
IN_SIZES = (FOX_W, FOX_W, FOX_W, FOX_HEADS,
            DIL_W, DIL_W, DIL_W,
            RET_W, RET_W, RET_W, RET_W,
            MLA_Q_RANK, MLA_KV_RANK, MLA_ROPE)
IN_WIDTH = sum(IN_SIZES)
IN_OFFSETS = tuple(int(v) for v in np.cumsum(IN_SIZES)[:-1])

kernel_name = 'hybrid_gated_fox_dilated_retention_mla_block'


def rms_norm(x, g):
    xf = x.astype(jnp.float32)
    y = xf * lax.rsqrt(jnp.mean(xf * xf, axis=-1, keepdims=True) + RMS_EPS)
    return (y * g.astype(jnp.float32)).astype(x.dtype)


def heads(t, n):
    B, S, _ = t.shape
    return t.reshape(B, S, n, -1).transpose(0, 2, 1, 3)


def merge_heads(t):
    B, H, S, d = t.shape
    return t.transpose(0, 2, 1, 3).reshape(B, S, H * d)


def rope(x, pos):
    half = x.shape[-1] // 2
    inv = ROPE_THETA ** (-jnp.arange(half, dtype=jnp.float32) / half)
    ang = pos.astype(jnp.float32)[:, None] * inv[None, :]
    cos, sin = jnp.cos(ang).astype(x.dtype), jnp.sin(ang).astype(x.dtype)
    x1, x2 = x[..., :half], x[..., half:]
    return jnp.concatenate([x1 * cos - x2 * sin, x1 * sin + x2 * cos], axis=-1)


def causal_block_attention(q, k, v, scale, decay=None):
    B, H, S, dk = q.shape
    nb = S // Q_BLOCK
    kpos = jnp.arange(S)
    qb = jnp.moveaxis(q.reshape(B, H, nb, Q_BLOCK, dk), 2, 0)
    idx = jnp.arange(nb)
    if decay is None:
        xs = (idx, qb)
    else:
        xs = (idx, qb, jnp.moveaxis(decay.reshape(B, H, nb, Q_BLOCK), 2, 0))

    def block(args):
        i, qi = args[0], args[1]
        s = jnp.einsum('bhqd,bhkd->bhqk', qi, k).astype(jnp.float32) * scale
        if decay is not None:
            s = s + args[2][..., :, None] - decay[..., None, :]
        qpos = i * Q_BLOCK + jnp.arange(Q_BLOCK)
        s = jnp.where(kpos[None, :] <= qpos[:, None], s, -jnp.inf)
        p = jax.nn.softmax(s, axis=-1).astype(v.dtype)
        return jnp.einsum('bhqk,bhkd->bhqd', p, v)

    o = lax.map(block, xs)
    return jnp.moveaxis(o, 0, 2).reshape(B, H, S, v.shape[-1])


def banded_window_attention(q, k, v, n, scale):
    *lead, L, d = q.shape
    nb = -(-L // n)
    pad_cfg = [(0, 0)] * len(lead) + [(0, nb * n - L), (0, 0)]

    def blocks(t):
        return jnp.pad(t, pad_cfg).reshape(*lead, nb, n, t.shape[-1])

    def with_prev(t):
        prev = jnp.concatenate([jnp.zeros_like(t[..., :1, :, :]), t[..., :-1, :, :]], axis=-3)
        return jnp.concatenate([prev, t], axis=-2)

    qb = blocks(q)
    kk, vv = with_prev(blocks(k)), with_prev(blocks(v))
    s = jnp.einsum('...qd,...kd->...qk', qb, kk).astype(jnp.float32) * scale
    qi = jnp.arange(n)
    ki = jnp.arange(2 * n) - n
    dist = qi[:, None] - ki[None, :]
    kabs = jnp.arange(nb)[:, None, None] * n + ki[None, None, :]
    valid = (dist >= 0) & (dist <= n) & (kabs >= 0)
    s = jnp.where(valid, s, -jnp.inf)
    lse = jax.nn.logsumexp(s, axis=-1)
    p = jnp.exp(s - lse[..., None]).astype(v.dtype)
    o = jnp.einsum('...qk,...kd->...qd', p, vv)
    o = o.reshape(*lead, nb * n, v.shape[-1])[..., :L, :]
    lse = lse.reshape(*lead, nb * n)[..., :L]
    return o, lse


def dilated_attention(q, k, v):
    B, H, S, d = q.shape
    scale = d ** -0.5
    outs, lses = [], []
    for window, dil in DIL_PATTERNS:
        L = S // dil

        def to_res(t):
            return t.reshape(B, H, L, dil, t.shape[-1]).transpose(0, 1, 3, 2, 4)

        o, lse = banded_window_attention(to_res(q), to_res(k), to_res(v), window // dil, scale)
        outs.append(o.transpose(0, 1, 3, 2, 4).reshape(B, H, S, d))
        lses.append(lse.transpose(0, 1, 3, 2).reshape(B, H, S))
    w = jax.nn.softmax(jnp.stack(lses), axis=0)
    return jnp.einsum('pbhs,pbhsd->bhsd', w.astype(q.dtype), jnp.stack(outs))


def retention(q, k, v):
    B, H, S, dk = q.shape
    dv = v.shape[-1]
    C = RET_CHUNK
    nc = S // C
    log_g = jnp.log1p(-(2.0 ** (-5.0 - jnp.arange(H, dtype=jnp.float32))))
    pos = jnp.arange(C, dtype=jnp.float32)
    rel = pos[:, None] - pos[None, :]
    d_in = jnp.where(rel >= 0, jnp.exp(jnp.maximum(rel, 0.0)[None] * log_g[:, None, None]), 0.0)
    zeta = jnp.exp((C - 1 - pos)[None, :] * log_g[:, None])
    xi = jnp.exp((pos + 1)[None, :] * log_g[:, None])
    g_chunk = jnp.exp(C * log_g)
    qc = q.astype(jnp.float32).reshape(B, H, nc, C, dk)
    kc = k.astype(jnp.float32).reshape(B, H, nc, C, dk)
    vc = v.astype(jnp.float32).reshape(B, H, nc, C, dv)
    scores = jnp.einsum('bhcnd,bhcmd->bhcnm', qc, kc) * d_in[None, :, None]
    inner = jnp.einsum('bhcnm,bhcme->bhcne', scores, vc)
    kv = jnp.einsum('bhcmd,bhcme->bhcde', kc * zeta[None, :, None, :, None], vc)

    def step(R, xs):
        q_i, kv_i = xs
        out = jnp.einsum('bhnd,bhde->bhne', q_i, R)
        return g_chunk[None, :, None, None] * R + kv_i, out

    R0 = jnp.zeros((B, H, dk, dv), jnp.float32)
    _, cross = lax.scan(step, R0, (jnp.moveaxis(qc, 2, 0), jnp.moveaxis(kv, 2, 0)))
    cross = jnp.moveaxis(cross, 0, 2) * xi[None, :, None, :, None]
    return (inner + cross).reshape(B, H, S, dv)


def hybrid_mixer(h, w_in, b_forget, ret_gn_gain, mla_q_norm, mla_kv_norm, w_uq, w_ukv, w_gate, w_branch, w_out):
    B, S, _ = h.shape
    pos = jnp.arange(S)
    (fq, fk, fv, ff, dq, dk, dv, rq, rk, rv, rg, cq, ckv, kr) = jnp.split(h @ w_in, IN_OFFSETS, axis=-1)

    log_f = jax.nn.log_sigmoid((ff + b_forget).astype(jnp.float32))
    decay = jnp.cumsum(log_f, axis=1).transpose(0, 2, 1)
    o_a = causal_block_attention(heads(fq, FOX_HEADS), heads(fk, FOX_HEADS), heads(fv, FOX_HEADS),
                                 HEAD_DIM ** -0.5, decay)

    o_b = dilated_attention(heads(dq, DIL_HEADS), heads(dk, DIL_HEADS), heads(dv, DIL_HEADS))

    o_c = retention(rope(heads(rq, RET_HEADS), pos), rope(heads(rk, RET_HEADS), pos) * HEAD_DIM ** -0.5,
                    heads(rv, RET_HEADS))
    mu = jnp.mean(o_c, axis=-1, keepdims=True)
    var = jnp.mean(jnp.square(o_c - mu), axis=-1, keepdims=True)
    o_c = merge_heads((o_c - mu) * lax.rsqrt(var + GN_EPS)).astype(h.dtype) * ret_gn_gain
    o_c = jax.nn.silu(rg) * o_c

    q = heads(rms_norm(cq, mla_q_norm) @ w_uq, MLA_HEADS)
    kv = heads(rms_norm(ckv, mla_kv_norm) @ w_ukv, MLA_HEADS)
    k_rope = jnp.broadcast_to(rope(kr[:, None], pos), (B, MLA_HEADS, S, MLA_ROPE))
    q_mla = jnp.concatenate([q[..., :MLA_NOPE], rope(q[..., MLA_NOPE:], pos)], axis=-1)
    k_mla = jnp.concatenate([kv[..., :MLA_NOPE], k_rope], axis=-1)
    o_d = causal_block_attention(q_mla, k_mla, kv[..., MLA_NOPE:], (MLA_NOPE + MLA_ROPE) ** -0.5)

    branches = jnp.stack([merge_heads(o_a), merge_heads(o_b), o_c, merge_heads(o_d)], axis=2)
    gates = jax.nn.sigmoid(h @ w_gate).reshape(B, S, N_BRANCH, D_MODEL)
    merged = jnp.sum(jnp.einsum('bsnw,nwd->bsnd', branches, w_branch) * gates, axis=2)
    return merged @ w_out


def setup_inputs(seed: int = 0) -> dict:
    key = jax.random.key(seed)
    ks = jax.random.split(key, 20)

    def nrm(k, shape, fan_in):
        return jax.random.normal(k, shape, jnp.float32) * fan_in ** -0.5

    def gain(k, shape):
        return 1.0 + 0.05 * jax.random.normal(k, shape, jnp.float32)

    L = DEPTH
    return {
        'x': jax.random.normal(ks[0], (BATCH, SEQ, D_MODEL), jnp.float32),
        'w_in': nrm(ks[1], (L, D_MODEL, IN_WIDTH), D_MODEL),
        'b_forget': 0.1 * jax.random.normal(ks[2], (L, FOX_HEADS), jnp.float32),
        'ret_gn_gain': gain(ks[3], (L, RET_W)),
        'mla_q_norm': gain(ks[4], (L, MLA_Q_RANK)),
        'mla_kv_norm': gain(ks[5], (L, MLA_KV_RANK)),
        'w_uq': nrm(ks[6], (L, MLA_Q_RANK, MLA_HEADS * (MLA_NOPE + MLA_ROPE)), MLA_Q_RANK),
        'w_ukv': nrm(ks[7], (L, MLA_KV_RANK, MLA_HEADS * (MLA_NOPE + MLA_V)), MLA_KV_RANK),
        'w_gate': nrm(ks[8], (L, D_MODEL, N_BRANCH * D_MODEL), D_MODEL),
        'w_branch': nrm(ks[9], (L, N_BRANCH, BRANCH_WIDTH, D_MODEL), BRANCH_WIDTH),
        'w_out': nrm(ks[10], (L, D_MODEL, D_MODEL), D_MODEL),
        'g_pre_mix': gain(ks[11], (L, D_MODEL)),
        'g_post_mix': gain(ks[12], (L, D_MODEL)),
        'g_pre_ffn': gain(ks[13], (L, D_MODEL)),
        'g_post_ffn': gain(ks[14], (L, D_MODEL)),
        'w_ffn_gate': nrm(ks[15], (L, D_MODEL, D_FF), D_MODEL),
        'w_ffn_up': nrm(ks[16], (L, D_MODEL, D_FF), D_MODEL),
        'w_ffn_down': nrm(ks[17], (L, D_FF, D_MODEL), D_FF),
    }


def reference(x, w_in, b_forget, ret_gn_gain, mla_q_norm, mla_kv_norm, w_uq, w_ukv, w_gate, w_branch, w_out,
              g_pre_mix, g_post_mix, g_pre_ffn, g_post_ffn, w_ffn_gate, w_ffn_up, w_ffn_down):
    for l in range(DEPTH):
        h = rms_norm(x, g_pre_mix[l])
        mix = hybrid_mixer(h, w_in[l], b_forget[l], ret_gn_gain[l], mla_q_norm[l], mla_kv_norm[l],
                           w_uq[l], w_ukv[l], w_gate[l], w_branch[l], w_out[l])
        x = x + rms_norm(mix, g_post_mix[l])
        h = rms_norm(x, g_pre_ffn[l])
        f = (jax.nn.silu(h @ w_ffn_gate[l]) * (h @ w_ffn_up[l])) @ w_ffn_down[l]
        x = x + rms_norm(f, g_post_ffn[l])
    return x
```

```python
import numpy as np
import concourse.bass as bass
import concourse.mybir as mybir

F32 = mybir.dt.float32
BF16 = mybir.dt.bfloat16
AF = mybir.ActivationFunctionType
ALU = mybir.AluOpType
AX = mybir.AxisListType
_DT_SIZE = {F32: 4, BF16: 2}


class Buf:
    __slots__ = ("name", "w", "r", "psum")

    def __init__(self, name, psum=False):
        self.name = name
        self.w = None
        self.r = []
        self.psum = psum


class Op:
    __slots__ = ("eng", "fn", "deps", "key", "grp", "needed", "tok", "line")

    def __init__(self, eng, fn, key, grp):
        self.eng = eng
        self.fn = fn
        self.deps = set()
        self.key = key
        self.grp = grp
        self.needed = False
        self.tok = None


class Pool:
    def __init__(self, items):
        self.items = items
        self.i = 0

    def next(self):
        it = self.items[self.i % len(self.items)]
        self.i += 1
        return it


class Prog:
    ENGS = ("pe", "act", "dve", "pool", "sp")

    def __init__(self, nc, sbuf_bytes=200000):
        self.nc = nc
        self.ops = []
        self.last = {}
        self.key_last = {}
        self.key_n = {}
        self.bar_id = 0
        self.bar_deps = []
        self.seen = {e: 0 for e in self.ENGS}
        self.cur_grp = None
        lo, hi = nc.bump_sbuf(sbuf_bytes)
        self.sb_lo, self.sb_hi = lo, hi
        self.sb_persist = lo
        self.sb_ptr = lo
        self.n_alloc = 0
        self.psum_f = []
        self.psum_b = []
        self.sb_peak = 0

    def alloc(self, name, shape, dtype, persist=False):
        nbytes = int(np.prod(shape[1:])) * _DT_SIZE[dtype]
        nbytes = (nbytes + 63) // 64 * 64
        off = self.sb_ptr
        if off + nbytes > self.sb_hi:
            raise RuntimeError(f"SBUF overflow allocating {name} {shape}: need {off + nbytes - self.sb_lo} > {self.sb_hi - self.sb_lo}")
        self.sb_ptr += nbytes
        self.sb_peak = max(self.sb_peak, self.sb_ptr - self.sb_lo)
        if persist:
            assert self.sb_persist == off, "persistent allocs must come first"
            self.sb_persist = self.sb_ptr
        self.n_alloc += 1
        t = self.nc.alloc_sbuf_tensor_at(f"{name}_{self.n_alloc}", list(shape), dtype, offset=off)
        return t, Buf(name)

    def pool(self, name, shape, dtype, n):
        return Pool([self.alloc(f"{name}{i}", shape, dtype) for i in range(n)])

    def skey(self):
        self._sk = getattr(self, "_sk", 0) + 1
        return f"st{self._sk % 8}"

    def phase_begin(self):
        self.barrier()
        self.sb_ptr = self.sb_persist

    def init_psum(self, nf32=6, nbf=2):
        for i in range(nf32):
            self.psum_f.append((self.nc.alloc_psum_tensor(f"psf{i}", [128, 512], F32), Buf(f"psf{i}", True)))
        for i in range(nbf):
            self.psum_b.append((self.nc.alloc_psum_tensor(f"psb{i}", [128, 1024], BF16), Buf(f"psb{i}", True)))
        self.pf = Pool(self.psum_f)
        self.pb = Pool(self.psum_b)

    def barrier(self):
        deps = set(v for v in self.last.values() if v is not None)
        deps.update(self.key_last.values())
        self.bar_deps = sorted(deps)
        self.bar_id += 1

    def add(self, eng, fn, R=(), W=(), key=None):
        idx = len(self.ops)
        op = Op(eng, fn, key, self.cur_grp if key is not None else None)
        import sys as _sys
        op.line = _sys._getframe(2).f_lineno
        deps = set()
        for b in R:
            if b.w is not None:
                deps.add((b.w, 0))
            if b.psum:
                for r in b.r:
                    if self.ops[r].eng != eng:
                        deps.add((r, 2))
        for b in W:
            if b.w is not None:
                deps.add((b.w, 1))
            for r in b.r:
                deps.add((r, 2))
        for d, kind in deps:
            od = self.ops[d]
            if od.key is None and key is None and od.eng == eng:
                if kind == 0 and eng != "pe":
                    op.deps.add(d)
            else:
                op.deps.add(d)
        if key is not None:
            kl = self.key_last.get(key)
            if kl is not None:
                if not (self.cur_grp is not None and self.ops[kl].grp is self.cur_grp):
                    op.deps.add(kl)
                else:
                    pass
            self.key_last[key] = idx
        if self.seen[eng] < self.bar_id:
            op.deps.update(self.bar_deps)
            self.seen[eng] = self.bar_id
        op.deps.discard(idx)
        if key is not None and self.cur_grp is not None:
            op.deps = set(d for d in op.deps if self.ops[d].grp is not self.cur_grp)
        for b in R:
            if key is None:
                b.r = [r for r in b.r if not (self.ops[r].key is None and self.ops[r].eng == eng)]
            b.r.append(idx)
        for b in W:
            b.w = idx
            b.r = []
        self.last[eng] = idx
        self.ops.append(op)
        return idx

    class _Grp:
        def __init__(self, P):
            self.P = P
            self.last_n = None

        def __enter__(self):
            assert self.P.cur_grp is None
            self.P.cur_grp = self
            return self

        def __exit__(self, *a):
            self.P.cur_grp = None

    def group(self):
        return Prog._Grp(self)

    def mm(self, out, lhsT, rhs, start, stop, R, W, skip=False):
        if skip:
            self.add("pe", lambda e: e.matmul(out, lhsT, rhs, start=start, stop=stop, skip_group_check=True), R, W)
        else:
            self.add("pe", lambda e: e.matmul(out, lhsT, rhs, start=start, stop=stop), R, W)

    def tr(self, out, in_, ident, R, W):
        self.add("pe", lambda e: e.transpose(out, in_, ident), R, W)

    def act(self, out, in_, func, R, W, bias=None, scale=None, accum=None):
        kw = {}
        if bias is not None:
            kw["bias"] = bias
        if scale is not None:
            kw["scale"] = scale
        if accum is not None:
            kw["accum_out"] = accum
        self.add("act", lambda e: e.activation(out, in_, func, **kw), R, W)

    def ts(self, out, in0, s1, s2, op0, op1, R, W, eng="dve", accum=None):
        kw = {}
        if accum is not None:
            kw["accum_out"] = accum
        if op1 is None:
            self.add(eng, lambda e: e.tensor_scalar(out, in0, s1, None, op0, **kw), R, W)
        else:
            self.add(eng, lambda e: e.tensor_scalar(out, in0, s1, s2, op0, op1, **kw), R, W)

    def tt(self, out, in0, in1, op, R, W, eng="dve"):
        self.add(eng, lambda e: e.tensor_tensor(out, in0, in1, op), R, W)

    def stt(self, out, in0, scalar, in1, op0, op1, R, W, eng="dve"):
        self.add(eng, lambda e: e.scalar_tensor_tensor(out, in0, scalar, in1, op0, op1), R, W)

    def cp(self, out, in_, R, W, eng="dve"):
        if eng == "act":
            self.add("act", lambda e: e.activation(out, in_, AF.Copy), R, W)
        else:
            self.add(eng, lambda e: e.tensor_copy(out, in_), R, W)

    def red(self, out, in_, op, R, W):
        self.add("dve", lambda e: e.tensor_reduce(out, in_, AX.X, op), R, W)

    def recip(self, out, in_, R, W):
        self.add("dve", lambda e: e.reciprocal(out, in_), R, W)

    def scan(self, out, d0, d1, initial, op0, op1, R, W):
        self.add("dve", lambda e: e.tensor_tensor_scan(out, d0, d1, initial, op0, op1), R, W)

    def memset(self, ap, val, W, eng="pool"):
        self.add(eng, lambda e: e.memset(ap, val), (), W)

    def dma(self, out, in_, R, W, key, q="sp", slow=False):
        if slow:
            self.add(q, lambda e: e.dma_start(out=out, in_=in_, allow_slow_non_contiguous=True), R, W, key=key)
        else:
            self.add(q, lambda e: e.dma_start(out=out, in_=in_), R, W, key=key)

    def emit(self):
        nc = self.nc
        ops = self.ops
        for op in ops:
            for d in op.deps:
                ops[d].needed = True
        for en in ("pe", "act", "dve", "pool"):
            for op in reversed(ops):
                if op.eng == en and op.key is None:
                    op.needed = True
                    break
        keys = sorted(set(op.key for op in ops if op.key is not None))
        from contextlib import ExitStack

        with ExitStack() as st:
            sem_eng = {e: st.enter_context(nc.semaphore(f"s_{e}")) for e in ("pe", "act", "dve", "pool")}
            sem_key = {k: st.enter_context(nc.semaphore(f"k_{k}")) for k in keys}
            cnt = {e: 0 for e in sem_eng}
            kcnt = {k: 0 for k in keys}
            grp_ops = {}
            for op in ops:
                if op.key is not None:
                    kcnt[op.key] += 1
                    op.tok = (sem_key[op.key], 16 * kcnt[op.key])
                    if op.grp is not None:
                        grp_ops.setdefault(id(op.grp), []).append(op)
                elif op.needed:
                    cnt[op.eng] += 1
                    op.tok = (sem_eng[op.eng], cnt[op.eng])
            for lst in grp_ops.values():
                last = lst[-1].tok
                for o in lst:
                    o.tok = last
            self.stats = dict(n_ops=len(ops), cnt=dict(cnt), n_keys=len(keys), sb_peak=self.sb_peak)
            n_waits = {e: 0 for e in self.ENGS}
            blk = st.enter_context(nc.Block())

            def make(eng):
                def body(e):
                    waited = {}
                    for op in ops:
                        if op.eng != eng:
                            continue
                        for d in sorted(op.deps):
                            sem, val = ops[d].tok
                            sid = id(sem)
                            if waited.get(sid, 0) < val:
                                e.wait_ge(sem, val)
                                waited[sid] = val
                                n_waits[eng] += 1
                        try:
                            ins = op.fn(e)
                        except Exception:
                            print("EMIT FAIL at recorded line", op.line, "eng", op.eng)
                            raise
                        if op.key is not None:
                            ins.then_inc(op.tok[0], 16)
                        elif op.needed:
                            ins.then_inc(op.tok[0], 1)
                    if eng == "sp":
                        for k in keys:
                            if waited.get(id(sem_key[k]), 0) < 16 * kcnt[k]:
                                e.wait_ge(sem_key[k], 16 * kcnt[k])
                        for en, c in cnt.items():
                            if c and waited.get(id(sem_eng[en]), 0) < c:
                                e.wait_ge(sem_eng[en], c)
                return body

            blk.tensor(make("pe"))
            blk.scalar(make("act"))
            blk.vector(make("dve"))
            blk.gpsimd(make("pool"))
            blk.sync(make("sp"))
            self.stats["waits"] = n_waits


import ml_dtypes
from concourse.bass_utils import run_bass_kernel_spmd

D = 1024
INW = 2980
DFF = 2816
FC = DFF // 128
EPS = 1e-6
GN_EPS = 1e-5
NEG = -30000.0
O_FQ, O_FK, O_FV, O_FF, O_DQ, O_DK, O_DV, O_RQ, O_RK, O_RV, O_RG, O_CQ, O_CKV, O_KR = (
    0, 256, 512, 768, 772, 1028, 1284, 1540, 1796, 2052, 2308, 2564, 2820, 2948)


def sl(start, n, step=1):
    return slice(start, start + step * (n - 1) + 1, step)


def host_consts(S):
    bf = ml_dtypes.bfloat16
    t = np.arange(S, dtype=np.float32)
    inv = (np.float32(10000.0) ** (-np.arange(32, dtype=np.float32) / np.float32(32))).astype(np.float32)
    ang = (t[:, None] * inv[None, :]).astype(np.float32)
    cos, sin = np.cos(ang).astype(np.float32), np.sin(ang).astype(np.float32)
    C = np.tile(np.concatenate([cos, cos], 1), (1, 4))
    Ss = np.tile(np.concatenate([-sin, sin], 1), (1, 4))
    c_rope = np.stack([C, Ss], 1).astype(np.float32)
    inv16 = (np.float32(10000.0) ** (-np.arange(16, dtype=np.float32) / np.float32(16))).astype(np.float32)
    ang = (t[:, None] * inv16[None, :]).astype(np.float32)
    cos, sin = np.cos(ang).astype(np.float32), np.sin(ang).astype(np.float32)
    Cm = np.tile(np.concatenate([cos, cos], 1), (1, 4))
    Sm = np.tile(np.concatenate([-sin, sin], 1), (1, 4))
    c_ropem = np.stack([Cm, Sm], 1).astype(np.float32)
    log_g = np.log1p(-(2.0 ** (-5.0 - np.arange(4, dtype=np.float64))))
    pos = np.arange(128, dtype=np.float64)
    XI = np.repeat(np.exp((pos[:, None] + 1) * log_g[None, :]), 64, axis=1)
    ZE = np.repeat(np.exp((127 - pos[:, None]) * log_g[None, :]), 64, axis=1) * 0.125
    rel = pos[None, :] - pos[:, None]
    DT = [np.where(rel >= 0, np.exp(np.maximum(rel, 0) * log_g[h]), 0.0) for h in range(4)]
    DT4 = np.concatenate(DT, 1)
    G = np.repeat(np.exp(128 * log_g)[None, :], 64, axis=1).repeat(64, axis=0)
    c_ret = np.concatenate([XI, ZE, DT4], 1).astype(np.float32)
    c_g = G.astype(np.float32)
    k = np.arange(128)[:, None]
    q = np.arange(128)[None, :]
    c_mask = np.concatenate([np.where(k > q, NEG, 0.0), np.where(k < q, NEG, 0.0)], 1).astype(bf)
    c_ident = np.eye(128).astype(bf)
    return dict(c_rope=c_rope, c_ropem=c_ropem, c_ret=c_ret, c_g=c_g, c_mask=c_mask, c_ident=c_ident)


def build(S, DEPTH, taps=(), stop_after=None):
    NT, NG = S // 128, S // 512
    nc = bass.Bass("TRN2", target_bir_lowering=False)

    def din(name, shape, dt=F32):
        return nc.dram_tensor(name, list(shape), dt, kind="ExternalInput").ap()

    L = DEPTH
    x_in = din("x", [S, D])
    w_in = din("w_in", [L, D, INW])
    b_forget = din("b_forget", [L, 4])
    ret_gn_gain = din("ret_gn_gain", [L, 256])
    mla_q_norm = din("mla_q_norm", [L, 256])
    mla_kv_norm = din("mla_kv_norm", [L, 128])
    w_uq = din("w_uq", [L, 256, 384])
    w_ukv = din("w_ukv", [L, 128, 512])
    w_gate = din("w_gate", [L, D, 4096])
    w_branch = din("w_branch", [L, 4, 256, D])
    w_out = din("w_out", [L, D, D])
    g_pre_mix = din("g_pre_mix", [L, D])
    g_post_mix = din("g_post_mix", [L, D])
    g_pre_ffn = din("g_pre_ffn", [L, D])
    g_post_ffn = din("g_post_ffn", [L, D])
    w_ffn_gate = din("w_ffn_gate", [L, D, DFF])
    w_ffn_up = din("w_ffn_up", [L, D, DFF])
    w_ffn_down = din("w_ffn_down", [L, DFF, D])
    c_rope = din("c_rope", [S, 2, 256])
    c_ropem = din("c_ropem", [S, 2, 128])
    c_ret = din("c_ret", [128, 1024])
    c_g = din("c_g", [64, 256])
    c_mask = din("c_mask", [128, 256], BF16)
    c_ident = din("c_ident", [128, 128], BF16)
    out = nc.dram_tensor("out", [S, D], F32, kind="ExternalOutput").ap()

    def scr(name, shape, dt):
        kind = "ExternalOutput" if name in taps else "Internal"
        return nc.dram_tensor(name, list(shape), dt, kind=kind).ap()

    fqk = scr("fqk", [4, 2, 70, S], BF16)
    dqk = scr("dqk", [4, 2, 64, S], BF16)
    mqk = scr("mqk", [4, 2, 96, S], BF16)
    fv = scr("fv", [S, 256], BF16)
    dv = scr("dv", [S, 256], BF16)
    mv = scr("mv", [S, 256], BF16)
    rv = scr("rv", [S, 256], BF16)
    rkz = scr("rkz", [S, 256], BF16)
    rg = scr("rg", [S, 256], F32)
    rqT = scr("rqT", [3, 4, 64, S], BF16)
    hTd = scr("hTd", [8, 128, S], BF16)
    brT = scr("brT", [8, 128, S], BF16)
    xa = scr("xa", [S, D], F32)
    xb = scr("xb", [S, D], F32)

    P = Prog(nc, sbuf_bytes=204800)
    P.init_psum()

    ident, ident_b = P.alloc("ident", [128, 128], BF16, persist=True)
    maskc, maskc_b = P.alloc("maskc", [128, 256], BF16, persist=True)
    gpre, gpre_b = P.alloc("gpre", [128, L * 2 * 8], F32, persist=True)
    mgn, mgn_b = P.alloc("mgn", [128, L * 3], F32, persist=True)
    negb, negb_b = P.alloc("negb", [4, L], F32, persist=True)
    P.dma(ident[:, :], c_ident[:, :], [], [ident_b], "cst")
    P.dma(maskc[:, :], c_mask[:, :], [], [maskc_b], "cst")
    for l in range(L):
        P.dma(gpre[:, (l * 2) * 8:(l * 2) * 8 + 8], g_pre_mix[l, :].rearrange("(c p) -> p c", p=128), [], [gpre_b], "cst", slow=True)
        P.dma(gpre[:, (l * 2 + 1) * 8:(l * 2 + 1) * 8 + 8], g_pre_ffn[l, :].rearrange("(c p) -> p c", p=128), [], [gpre_b], "cst", slow=True)
        P.dma(mgn[:, l * 3:l * 3 + 2], mla_q_norm[l, :].rearrange("(c p) -> p c", p=128), [], [mgn_b], "cst", slow=True)
        P.dma(mgn[:, l * 3 + 2:l * 3 + 3], mla_kv_norm[l, :].rearrange("(c p) -> p c", p=128), [], [mgn_b], "cst", slow=True)
    P.dma(negb[:, :], b_forget.rearrange("l h -> h l"), [], [negb_b], "cst", slow=True)
    P.ts(negb[:, :], negb[:, :], -1.0, None, ALU.mult, None, [negb_b], [negb_b])

    IDR = [ident_b]

    def phase_A(l, x_src):
        P.phase_begin()
        wA, wA_b = P.alloc("wA", [128, 8, INW], BF16)
        wuq, wuq_b = P.alloc("wuq", [128, 2, 384], BF16)
        wukv, wukv_b = P.alloc("wukv", [128, 512], BF16)
        for c in range(8):
            P.dma(wA[:, c, :], w_in[l, c * 128:(c + 1) * 128, :], [], [wA_b], f"w{c % 4}", q="pool")
        P.dma(wuq[:, :, :], w_uq[l].rearrange("(c p) n -> p c n", p=128), [], [wuq_b], "w0", q="pool")
        P.dma(wukv[:, :], w_ukv[l, :, :], [], [wukv_b], "w1", q="pool")
        cret, cret_b = P.alloc("cret", [128, 512], F32)
        P.dma(cret[:, :], c_ret[:, 0:512], [], [cret_b], "cst")
        xpool = P.pool("xt", [128, D], F32, 2)
        junk, junk_b = P.alloc("junk", [128, D], BF16)
        hnpool = P.pool("hn", [128, D], BF16, 2)
        hTpool = P.pool("hT", [128, 8, 512], BF16, 2)
        small = P.pool("sm", [128, 8], F32, 4)
        rtpool = P.pool("rt", [128, 2, 256], F32, 2)
        mrtpool = P.pool("mrt", [128, 2, 128], F32, 2)
        stF = P.pool("stF", [128, 512], BF16, 3)
        stV = P.pool("stV", [128, 512], BF16, 2)
        f32t = P.pool("f32t", [128, 256], F32, 6)
        bft = P.pool("bft", [128, 256], BF16, 8)
        rTst, rTst_b = P.alloc("rTst", [128, 6, 512], BF16)
        stRG = P.pool("stRG", [128, 256], F32, 2)
        mnb = P.pool("mnb", [128, 384], BF16, 2)
        mTp = P.pool("mTp", [128, 3, 128], BF16, 2)
        qtok = P.pool("qtok", [128, 4, 96], BF16, 2)
        ktok = P.pool("ktok", [128, 4, 96], BF16, 2)
        mtmp = P.pool("mtmp", [128, 128], F32, 4)
        krp = P.pool("krr", [128, 32], F32, 2)
        mst, mst_b = P.alloc("mst", [96, 8, 512], BF16)
        a2 = P.pool("a2", [4, 512], F32, 6)
        ccp = P.pool("cc", [4, 512], F32, 2)
        ones4, ones4_b = P.alloc("ones4", [4, 512], F32)
        Z, Z_b = P.alloc("Z", [4, 12, 512], BF16)
        P.memset(ones4[:, :], 1.0, [ones4_b])
        P.memset(Z[:, 3:9, :], 1.0, [Z_b])
        gcol = (l * 2) * 8
        cc_prev = None
        evi = [0]

        def evac(o, i, R, W, scale=None):
            evi[0] += 1
            if evi[0] % 2 == 0:
                if scale is None:
                    P.cp(o, i, R, W, eng="act")
                else:
                    P.act(o, i, AF.Copy, R, W, scale=scale)
            else:
                if scale is None:
                    P.cp(o, i, R, W)
                else:
                    P.ts(o, i, scale, None, ALU.mult, None, R, W)

        def rope(X, C, Sg, Cb, half, nh, hstride, o, o_b, XR):
            t1, t1_b = (mtmp if half == 16 else f32t).next()
            t2, t2_b = (mtmp if half == 16 else f32t).next()
            w = 2 * half
            n = nh * w
            Xv = X
            t1v = t1[:, 0:n].rearrange("p (h e) -> p h e", h=nh)
            t2v = t2[:, 0:n].rearrange("p (h e) -> p h e", h=nh)
            P.tt(t1v, Xv, C, ALU.mult, XR + Cb, [t1_b])
            P.tt(t2v[:, :, 0:half], Xv[:, :, half:w], Sg[:, :, 0:half], ALU.mult, XR + Cb, [t2_b])
            P.tt(t2v[:, :, half:w], Xv[:, :, 0:half], Sg[:, :, half:w], ALU.mult, XR + Cb, [t2_b])
            P.tt(o, t1v, t2v, ALU.add, [t1_b, t2_b], [o_b])

        for g in range(NG):
            hTg, hTg_b = hTpool.next()
            gs = slice(g * 512, (g + 1) * 512)
            for t in range(4):
                tok0 = g * 512 + t * 128
                xt, xt_b = xpool.next()
                P.dma(xt[:, :], x_src[tok0:tok0 + 128, :], [], [xt_b], f"x{xpool.i % 2}")
                ss, ss_b = small.next()
                P.act(junk[:, :], xt[:, :], AF.Square, [xt_b], [junk_b, ss_b], accum=ss[:, 0:1])
                P.act(ss[:, 1:2], ss[:, 0:1], AF.Ln, [ss_b], [ss_b], scale=1.0 / D, bias=EPS)
                P.act(ss[:, 2:3], ss[:, 1:2], AF.Exp, [ss_b], [ss_b], scale=-0.5)
                hn, hn_b = hnpool.next()
                P.act(hn[:, :], xt[:, :], AF.Copy, [xt_b, ss_b], [hn_b], scale=ss[:, 2:3])
                pT, pT_b = P.pb.next()
                for c in range(8):
                    P.tr(pT[:, c * 128:(c + 1) * 128], hn[:, c * 128:(c + 1) * 128], ident[:, :], [hn_b] + IDR, [pT_b])
                P.tt(hTg[:, :, t * 128:(t + 1) * 128], pT[:, :].rearrange("p (c k) -> p c k", c=8),
                     gpre[:, gcol:gcol + 8].unsqueeze(2).to_broadcast([128, 8, 128]), ALU.mult, [pT_b, gpre_b], [hTg_b])
            P.dma(hTd[:, :, gs].rearrange("c p s -> p c s"), hTg[:, :, :], [hTg_b], [], "sthT", q="pool")
            for (col, dst, qk, j, scale) in ((O_FQ, fqk, 0, 0, 0.125), (O_FQ + 128, fqk, 0, 1, 0.125),
                                            (O_FK, fqk, 1, 0, None), (O_FK + 128, fqk, 1, 1, None),
                                            (O_DQ, dqk, 0, 0, 0.125), (O_DQ + 128, dqk, 0, 1, 0.125),
                                            (O_DK, dqk, 1, 0, None), (O_DK + 128, dqk, 1, 1, None)):
                ps, ps_b = P.pf.next()
                for c in range(8):
                    P.mm(ps[:, :], wA[:, c, col:col + 128], hTg[:, c, :], c == 0, c == 7, [wA_b, hTg_b], [ps_b])
                st, st_b = stF.next()
                evac(st[:, :], ps[:, :], [ps_b], [st_b], scale)
                sk_ = P.skey()
                with P.group():
                    for hh in range(2):
                        P.dma(dst[2 * j + hh, qk, 0:64, gs], st[hh * 64:(hh + 1) * 64, :], [st_b], [], sk_, q="pool")
            ps, ps_b = P.pf.next()
            for c in range(8):
                P.mm(ps[0:4, :], wA[:, c, O_FF:O_FF + 4], hTg[:, c, :], c == 0, c == 7, [wA_b, hTg_b], [ps_b])
            e1, e1_b = a2.next()
            l1, l1_b = a2.next()
            P.act(e1[:, :], ps[0:4, :], AF.Exp, [ps_b, negb_b], [e1_b], scale=-1.0, bias=negb[:, l:l + 1])
            P.act(l1[:, :], e1[:, :], AF.Ln, [e1_b], [l1_b], bias=1.0)
            cc, cc_b = ccp.next()
            if cc_prev is None:
                P.scan(cc[:, :], ones4[:, :], l1[:, :], 0.0, ALU.mult, ALU.subtract, [ones4_b, l1_b], [cc_b])
            else:
                P.scan(cc[:, :], ones4[:, :], l1[:, :], cc_prev[0][:, 511:512], ALU.mult, ALU.subtract,
                       [ones4_b, l1_b, cc_prev[1]], [cc_b])
            cc_prev = (cc, cc_b)
            t32, t32_b = a2.next()
            r1, r1_b = a2.next()
            P.cp(Z[:, 0, :], cc[:, :], [cc_b], [Z_b])
            P.cp(t32[:, :], Z[:, 0, :], [Z_b], [t32_b])
            P.tt(r1[:, :], cc[:, :], t32[:, :], ALU.subtract, [cc_b, t32_b], [r1_b])
            P.cp(Z[:, 1, :], r1[:, :], [r1_b], [Z_b])
            P.cp(t32[:, :], Z[:, 1, :], [Z_b], [t32_b])
            P.tt(r1[:, :], r1[:, :], t32[:, :], ALU.subtract, [r1_b, t32_b], [r1_b])
            P.cp(Z[:, 2, :], r1[:, :], [r1_b], [Z_b])
            P.ts(Z[:, 9:12, :], Z[:, 0:3, :], -1.0, None, ALU.mult, None, [Z_b], [Z_b])
            with P.group():
                for h in range(4):
                    P.dma(fqk[h, 0:1, 64:70, gs], Z[h:h + 1, 0:6, :], [Z_b], [], "stZ", q="pool")
                    P.dma(fqk[h, 1:2, 64:70, gs], Z[h:h + 1, 6:12, :], [Z_b], [], "stZ", q="pool")
            for t in range(4):
                tok0 = g * 512 + t * 128
                ts_ = slice(tok0, tok0 + 128)
                tc = slice(t * 128, (t + 1) * 128)
                rt, rt_b = rtpool.next()
                P.dma(rt[:, :, :], c_rope[ts_, :, :], [], [rt_b], f"rt{rtpool.i % 2}")
                mrt, mrt_b = mrtpool.next()
                P.dma(mrt[:, :, :], c_ropem[ts_, :, :], [], [mrt_b], f"mrt{mrtpool.i % 2}")

                def tokmm(ps, ps_b, cols):
                    for (po, wo, n) in cols:
                        for c in range(8):
                            P.mm(ps[:, po:po + n], hTg[:, c, tc], wA[:, c, wo:wo + n], c == 0, c == 7, [wA_b, hTg_b], [ps_b])

                ps, ps_b = P.pf.next()
                tokmm(ps, ps_b, [(0, O_FV, 256), (256, O_DV, 256)])
                st, st_b = stV.next()
                evac(st[:, :], ps[:, :], [ps_b], [st_b])
                sk_ = P.skey()
                with P.group():
                    P.dma(fv[ts_, :], st[:, 0:256], [st_b], [], sk_, q="pool")
                    P.dma(dv[ts_, :], st[:, 256:512], [st_b], [], sk_, q="pool")
                ps, ps_b = P.pf.next()
                tokmm(ps, ps_b, [(0, O_RQ, 512)])
                Cv = rt[:, 0, :].rearrange("p (h e) -> p h e", h=4)
                Sv = rt[:, 1, :].rearrange("p (h e) -> p h e", h=4)
                outs_bf = []
                for which in range(2):
                    X = ps[:, which * 256:(which + 1) * 256].rearrange("p (h e) -> p h e", h=4)
                    qf, qf_b = f32t.next()
                    rope(X, Cv, Sv, [rt_b], 32, 4, 64, qf[:, :].rearrange("p (h e) -> p h e", h=4), qf_b, [ps_b])
                    a_, a_b = bft.next()
                    b_, b_b = bft.next()
                    if which == 0:
                        P.cp(a_[:, :], qf[:, :], [qf_b], [a_b], eng="act")
                        P.tt(b_[:, :], qf[:, :], cret[:, 0:256], ALU.mult, [qf_b, cret_b], [b_b])
                    else:
                        P.act(a_[:, :], qf[:, :], AF.Copy, [qf_b], [a_b], scale=0.125)
                        P.tt(b_[:, :], qf[:, :], cret[:, 256:512], ALU.mult, [qf_b, cret_b], [b_b])
                    outs_bf.append((a_, a_b, b_, b_b))
                (qa, qa_b, qb, qb_b), (ka, ka_b, kz, kz_b) = outs_bf
                P.dma(rkz[ts_, :], kz[:, :], [kz_b], [], P.skey(), q="pool")
                pT, pT_b = P.pb.next()
                for idx, (src, src_b) in enumerate(((qa, qa_b), (qb, qb_b), (ka, ka_b))):
                    for j in range(2):
                        P.tr(pT[:, (idx * 2 + j) * 128:(idx * 2 + j + 1) * 128], src[:, j * 128:(j + 1) * 128], ident[:, :], [src_b] + IDR, [pT_b])
                evac(rTst[:, :, tc], pT[:, 0:768].rearrange("p (c k) -> p c k", c=6), [pT_b], [rTst_b])
                ps, ps_b = P.pf.next()
                tokmm(ps, ps_b, [(0, O_RV, 512)])
                a_, a_b = bft.next()
                evac(a_[:, :], ps[:, 0:256], [ps_b], [a_b])
                P.dma(rv[ts_, :], a_[:, :], [a_b], [], P.skey(), q="pool")
                srg, srg_b = stRG.next()
                evac(srg[:, :], ps[:, 256:512], [ps_b], [srg_b])
                P.dma(rg[ts_, :], srg[:, :], [srg_b], [], P.skey(), q="pool")
                ps, ps_b = P.pf.next()
                tokmm(ps, ps_b, [(0, O_CQ, 416)])
                ss, ss_b = small.next()
                P.act(junk[:, 0:256], ps[:, 0:256], AF.Square, [ps_b], [junk_b, ss_b], accum=ss[:, 0:1])
                P.act(junk[:, 256:384], ps[:, 256:384], AF.Square, [ps_b], [junk_b, ss_b], accum=ss[:, 1:2])
                P.act(ss[:, 2:3], ss[:, 0:1], AF.Ln, [ss_b], [ss_b], scale=1.0 / 256, bias=EPS)
                P.act(ss[:, 3:4], ss[:, 1:2], AF.Ln, [ss_b], [ss_b], scale=1.0 / 128, bias=EPS)
                P.act(ss[:, 4:6], ss[:, 2:4], AF.Exp, [ss_b], [ss_b], scale=-0.5)
                mn, mn_b = mnb.next()
                P.act(mn[:, 0:256], ps[:, 0:256], AF.Copy, [ps_b, ss_b], [mn_b], scale=ss[:, 4:5])
                P.act(mn[:, 256:384], ps[:, 256:384], AF.Copy, [ps_b, ss_b], [mn_b], scale=ss[:, 5:6])
                pT, pT_b = P.pb.next()
                for c in range(3):
                    P.tr(pT[:, c * 128:(c + 1) * 128], mn[:, c * 128:(c + 1) * 128], ident[:, :], [mn_b] + IDR, [pT_b])
                mT, mT_b = mTp.next()
                P.tt(mT[:, :, :], pT[:, 0:384].rearrange("p (c k) -> p c k", c=3),
                     mgn[:, l * 3:l * 3 + 3].unsqueeze(2).to_broadcast([128, 3, 128]), ALU.mult, [pT_b, mgn_b], [mT_b])
                krr, krr_b = krp.next()
                Cm = mrt[:, 0, :].rearrange("p (h e) -> p h e", h=4)
                Sm = mrt[:, 1, :].rearrange("p (h e) -> p h e", h=4)
                rope(ps[:, 384:416].rearrange("p (h e) -> p h e", h=1), Cm[:, 0:1, :], Sm[:, 0:1, :], [mrt_b], 16, 1, 32,
                     krr[:, 0:32].rearrange("p (h e) -> p h e", h=1), krr_b, [ps_b])
                psq, psq_b = P.pf.next()
                for c in range(2):
                    P.mm(psq[:, 0:384], mT[:, c, :], wuq[:, c, :], c == 0, c == 1, [mT_b, wuq_b], [psq_b])
                pskv, pskv_b = P.pf.next()
                P.mm(pskv[:, :], mT[:, 2, :], wukv[:, :], True, True, [mT_b, wukv_b], [pskv_b])
                qt, qt_b = qtok.next()
                P.cp(qt[:, :, :], psq[:, 0:384].rearrange("p (h e) -> p h e", h=4), [psq_b], [qt_b], eng="act")
                rope(psq[:, 0:384].rearrange("p (h e) -> p h e", h=4)[:, :, 64:96], Cm, Sm, [mrt_b], 16, 4, 96,
                     qt[:, :, 64:96], qt_b, [psq_b])
                kt_, kt_b = ktok.next()
                kvv = pskv[:, :].rearrange("p (h e) -> p h e", h=4)
                P.cp(kt_[:, :, 0:64], kvv[:, :, 0:64], [pskv_b], [kt_b], eng="act")
                P.cp(kt_[:, :, 64:96], krr[:, 0:32].unsqueeze(1).to_broadcast([128, 4, 32]), [krr_b], [kt_b])
                a_, a_b = bft.next()
                P.cp(a_[:, :].rearrange("p (h e) -> p h e", h=4), kvv[:, :, 64:128], [pskv_b], [a_b])
                P.dma(mv[ts_, :], a_[:, :], [a_b], [], P.skey(), q="pool")
                pT, pT_b = P.pb.next()
                for h in range(4):
                    P.tr(pT[0:96, h * 128:(h + 1) * 128], qt[:, h, :], ident[:, :], [qt_b] + IDR, [pT_b])
                    P.tr(pT[0:96, (4 + h) * 128:(5 + h) * 128], kt_[:, h, :], ident[:, :], [kt_b] + IDR, [pT_b])
                evac(mst[:, :, tc], pT[0:96, :].rearrange("p (c k) -> p c k", c=8), [pT_b], [mst_b])
            with P.group():
                for w in range(3):
                    for j in range(2):
                        for hh in range(2):
                            P.dma(rqT[w, 2 * j + hh, :, gs], rTst[hh * 64:(hh + 1) * 64, w * 2 + j, :], [rTst_b], [], "strT", q="pool")
            with P.group():
                for h in range(4):
                    P.dma(mqk[h, 0, :, gs], mst[:, h, :], [mst_b], [], "stm", q="pool")
                    P.dma(mqk[h, 1, :, gs], mst[:, 4 + h, :], [mst_b], [], "stm", q="pool")

    def attn_full(qk, vsrc, K, act_scale, br):
        P.phase_begin()
        kpool = P.pool("kT", [96, S], BF16, 2)
        qpool = P.pool("qT", [96, S], BF16, 2)
        vpool = P.pool("va", [128, NT, 128], BF16, 2)
        for (va, va_b) in vpool.items:
            P.memset(va[:, :, 64:128], 1.0, [va_b])
        ptp = P.pool("pt", [128, 512], BF16, 4)
        recp = P.pool("rec", [64, 512], F32, 2)
        otp = P.pool("ot", [128, 512], BF16, 2)
        vsr = vsrc.rearrange("(t p) e -> p t e", p=128)
        accP = Pool(P.psum_f[0:2])
        scP = Pool(P.psum_f[2:6])
        for h in range(4):
            kT, kT_b = kpool.next()
            qT, qT_b = qpool.next()
            va, va_b = vpool.next()
            P.dma(kT[0:K, :], qk[h, 1, :, :], [], [kT_b], f"lk{h % 2}")
            P.dma(qT[0:K, :], qk[h, 0, :, :], [], [qT_b], f"lq{h % 2}")
            with P.group():
                hn_ = NT // 2
                P.dma(va[:, 0:hn_, 0:64], vsr[:, 0:hn_, h * 64:(h + 1) * 64], [], [va_b], f"lv{h % 2}")
                P.dma(va[:, hn_:NT, 0:64], vsr[:, hn_:NT, h * 64:(h + 1) * 64], [], [va_b], f"lv{h % 2}")
            for qg in range(NG):
                acc, acc_b = accP.next()
                nkt = 4 * qg + 4

                def qk_stage(kt):
                    j0 = max(0, kt - 4 * qg) * 128
                    diag = kt >= 4 * qg
                    sp_, sp_b = scP.next()
                    P.mm(sp_[:, j0:512], kT[0:K, kt * 128:(kt + 1) * 128], qT[0:K, qg * 512 + j0:(qg + 1) * 512],
                         True, not diag, [kT_b, qT_b], [sp_b], skip=True)
                    if diag:
                        P.mm(sp_[:, j0:j0 + 128], ident[:, :], maskc[:, 0:128], False, True, IDR + [maskc_b], [sp_b], skip=True)
                    pt, pt_b = ptp.next()
                    if act_scale is None:
                        P.act(pt[:, j0:512], sp_[:, j0:512], AF.Exp, [sp_b], [pt_b])
                    else:
                        P.act(pt[:, j0:512], sp_[:, j0:512], AF.Exp, [sp_b], [pt_b], scale=act_scale)
                    return pt, pt_b, j0

                def pv_stage(kt, pt, pt_b, j0):
                    P.mm(acc[:, j0:512], va[:, kt, :], pt[:, j0:512], kt == 0, kt == nkt - 1, [va_b, pt_b], [acc_b], skip=True)

                pend = [qk_stage(0)]
                if nkt > 1:
                    pend.append(qk_stage(1))
                for kt in range(nkt):
                    if kt + 2 < nkt:
                        pend.append(qk_stage(kt + 2))
                    pv_stage(kt, *pend.pop(0))
                rec, rec_b = recp.next()
                P.recip(rec[:, :], acc[64:128, :], [acc_b], [rec_b])
                ot, ot_b = otp.next()
                p0 = (h % 2) * 64
                P.tt(ot[p0:p0 + 64, :], acc[0:64, :], rec[:, :], ALU.mult, [acc_b, rec_b], [ot_b])
                P.dma(brT[br * 2 + h // 2, p0:p0 + 64, qg * 512:(qg + 1) * 512], ot[p0:p0 + 64, :], [ot_b], [], P.skey(), q="pool")

    def attn_dil():
        P.phase_begin()
        pats = (1, 4, 16)
        vd = [P.alloc(f"vd{p}", [128, NT, 4, 128], BF16) for p in range(3)]
        for p, r in enumerate(pats):
            v_, v_b = vd[p]
            P.memset(v_[:, :, :, 64:128], 1.0, [v_b])
            nb = NT // r
            src = dv.rearrange("(b j r) (h e) -> r j b h e", j=128, r=r, h=4)
            for c in range(r):
                with P.group():
                    for b0 in range(nb):
                        P.dma(v_[:, c * nb + b0, :, 0:64], src[c, :, b0, :, :], [], [v_b], f"lvd{p}")
        kpool = P.pool("kT", [64, S], BF16, 2)
        qpool = P.pool("qT", [64, S], BF16, 2)
        accp = P.pool("accS", [128, S], F32, 2)
        ptp = P.pool("pt", [128, 256], BF16, 4)
        recp = P.pool("rec", [64, 512], F32, 2)
        otp = P.pool("ot", [128, 512], BF16, 2)
        accP = Pool(P.psum_f[0:2])
        scP = Pool(P.psum_f[2:6])
        for h in range(4):
            kT, kT_b = kpool.next()
            qT, qT_b = qpool.next()
            aS, aS_b = accp.next()
            P.dma(kT[:, :], dqk[h, 1, :, :], [], [kT_b], f"lk{h % 2}")
            P.dma(qT[:, :], dqk[h, 0, :, :], [], [qT_b], f"lq{h % 2}")
            for p, r in enumerate(pats):
                v_, v_b = vd[p]
                nb = NT // r
                for c in range(r):
                    banks = {}

                    def bank(bi):
                        if bi not in banks:
                            banks[bi] = accP.next()
                        return banks[bi]

                    def qk_stage(b):
                        nq = 2 if b + 1 < nb else 1
                        ncol = 128 * nq
                        base = c + r * 128 * b
                        sp_, sp_b = scP.next()
                        P.mm(sp_[:, 0:ncol], kT[:, sl(base, 128, r)], qT[:, sl(base, ncol, r)], True, False, [kT_b, qT_b], [sp_b], skip=True)
                        P.mm(sp_[:, 0:ncol], ident[:, :], maskc[:, 0:ncol], False, True, IDR + [maskc_b], [sp_b], skip=True)
                        pt, pt_b = ptp.next()
                        P.act(pt[:, 0:ncol], sp_[:, 0:ncol], AF.Exp, [sp_b], [pt_b])
                        return pt, pt_b, nq

                    def pv_stage(b, pt, pt_b, nq):
                        bk, bk_b = bank(b // 4)
                        co = (b % 4) * 128
                        P.mm(bk[:, co:co + 128], v_[:, c * nb + b, h, :], pt[:, 0:128], b == 0, True, [v_b, pt_b], [bk_b], skip=True)
                        if nq == 2:
                            bk2, bk2_b = bank((b + 1) // 4)
                            co2 = ((b + 1) % 4) * 128
                            P.mm(bk2[:, co2:co2 + 128], v_[:, c * nb + b, h, :], pt[:, 128:256], True, False, [v_b, pt_b], [bk2_b], skip=True)
                        if b % 4 == 3 or b == nb - 1:
                            nblk = b % 4 + 1
                            cnt = nblk * 128
                            L0 = (b // 4) * 512
                            dst = aS[:, sl(c + r * L0, cnt, r)]
                            if p == 0:
                                P.cp(dst, bk[:, 0:cnt], [bk_b], [aS_b])
                            else:
                                P.tt(dst, dst, bk[:, 0:cnt], ALU.add, [aS_b, bk_b], [aS_b])

                    pend = [qk_stage(0)]
                    for b in range(nb):
                        if b + 1 < nb:
                            pend.append(qk_stage(b + 1))
                        pv_stage(b, *pend.pop(0))
            for qg in range(NG):
                gs = slice(qg * 512, (qg + 1) * 512)
                rec, rec_b = recp.next()
                P.recip(rec[:, :], aS[64:128, gs], [aS_b], [rec_b])
                ot, ot_b = otp.next()
                p0 = (h % 2) * 64
                P.tt(ot[p0:p0 + 64, :], aS[0:64, gs], rec[:, :], ALU.mult, [aS_b, rec_b], [ot_b])
                P.dma(brT[2 + h // 2, p0:p0 + 64, gs], ot[p0:p0 + 64, :], [ot_b], [], P.skey(), q="pool")

    def ret_phase(l):
        P.phase_begin()
        cdt, cdt_b = P.alloc("cdt", [128, 512], F32)
        cg, cg_b = P.alloc("cg", [64, 256], F32)
        gnb, gnb_b = P.alloc("gnb", [128, 256], F32)
        P.dma(cdt[:, :], c_ret[:, 512:1024], [], [cdt_b], "cst")
        P.dma(cg[:, :], c_g[:, :], [], [cg_b], "cst")
        P.dma(gnb[:, :], ret_gn_gain[l:l + 1, :].to_broadcast([128, 256]), [], [gnb_b], "cst")
        kz, kz_b = P.alloc("kz", [128, NT, 256], BF16)
        rvs, rvs_b = P.alloc("rvs", [128, NT, 256], BF16)
        P.dma(kz[:, :, :], rkz.rearrange("(t p) e -> p t e", p=128), [], [kz_b], "lk0")
        P.dma(rvs[:, :, :], rv.rearrange("(t p) e -> p t e", p=128), [], [rvs_b], "lv0")
        Rf, Rf_b = P.alloc("Rf", [64, 256], F32)
        Rb, Rb_b = P.alloc("Rb", [64, NT, 256], BF16)
        P.memset(Rf[:, :], 0.0, [Rf_b])
        P.memset(Rb[:, 0, :], 0.0, [Rb_b])
        for i in range(NT - 1):
            ps, ps_b = P.pf.next()
            for h in range(4):
                hs = slice(h * 64, (h + 1) * 64)
                P.mm(ps[0:64, hs], kz[:, i, hs], rvs[:, i, hs], True, True, [kz_b, rvs_b], [ps_b])
            P.tt(Rf[:, :], Rf[:, :], cg[:, :], ALU.mult, [Rf_b, cg_b], [Rf_b])
            P.tt(Rf[:, :], Rf[:, :], ps[0:64, 0:256], ALU.add, [Rf_b, ps_b], [Rf_b])
            P.cp(Rb[:, i + 1, :], Rf[:, :], [Rf_b], [Rb_b], eng="act")
        rqp = P.pool("rq", [64, 3, 4, 512], BF16, 2)
        rgp = P.pool("rgl", [128, 4, 256], F32, 2)
        sdp = P.pool("sd", [128, 512], BF16, 2)
        sqp = P.pool("sq", [128, 256], F32, 2)
        yp = P.pool("y", [128, 256], F32, 2)
        sgp = P.pool("sg", [128, 4, 256], F32, 2)
        ocp = P.pool("oc", [128, 256], BF16, 2)
        stp = P.pool("st8", [128, 16], F32, 4)
        ocT = P.pool("ocT", [128, 2, 512], BF16, 2)
        for g in range(NG):
            gs = slice(g * 512, (g + 1) * 512)
            rq, rq_b = rqp.next()
            with P.group():
                for w in range(3):
                    P.dma(rq[:, w, :, :], rqT[w, :, :, gs].rearrange("h d s -> d h s"), [], [rq_b], f"lq{g % 2}")
            rgl, rgl_b = rgp.next()
            P.dma(rgl[:, :, :], rg[gs, :].rearrange("(t p) e -> p t e", p=128), [], [rgl_b], f"lrg{g % 2}")
            oT, oT_b = ocT.next()
            sg, sg_b = sgp.next()
            P.act(sg[:, :, :], rgl[:, :, :], AF.Silu, [rgl_b], [sg_b])
            for t in range(4):
                i = g * 4 + t
                tc = slice(t * 128, (t + 1) * 128)
                sps, sps_b = P.pf.next()
                for h in range(4):
                    P.mm(sps[:, h * 128:(h + 1) * 128], rq[:, 2, h, tc], rq[:, 0, h, tc], True, True, [rq_b], [sps_b])
                sd, sd_b = sdp.next()
                P.tt(sd[:, :], sps[:, :], cdt[:, :], ALU.mult, [sps_b, cdt_b], [sd_b])
                ops_, ops_b = P.pf.next()
                for h in range(4):
                    hs = slice(h * 64, (h + 1) * 64)
                    P.mm(ops_[:, hs], sd[:, h * 128:(h + 1) * 128], rvs[:, i, hs], True, False, [sd_b, rvs_b], [ops_b])
                    P.mm(ops_[:, hs], rq[:, 1, h, tc], Rb[:, i, hs], False, True, [rq_b, Rb_b], [ops_b])
                st8, st8_b = stp.next()
                ov = ops_[:, 0:256].rearrange("p (h e) -> p h e", h=4)
                P.red(st8[:, 0:4], ov, ALU.add, [ops_b], [st8_b])
                sq, sq_b = sqp.next()
                P.act(sq[:, :], ops_[:, 0:256], AF.Square, [ops_b], [sq_b])
                P.red(st8[:, 4:8], sq[:, :].rearrange("p (h e) -> p h e", h=4), ALU.add, [sq_b], [st8_b])
                P.ts(st8[:, 0:4], st8[:, 0:4], 1.0 / 64, None, ALU.mult, None, [st8_b], [st8_b])
                P.tt(st8[:, 8:12], st8[:, 0:4], st8[:, 0:4], ALU.mult, [st8_b], [st8_b])
                P.stt(st8[:, 4:8], st8[:, 4:8], 1.0 / 64, st8[:, 8:12], ALU.mult, ALU.subtract, [st8_b], [st8_b])
                P.act(st8[:, 8:12], st8[:, 4:8], AF.Ln, [st8_b], [st8_b], bias=GN_EPS)
                P.act(st8[:, 12:16], st8[:, 8:12], AF.Exp, [st8_b], [st8_b], scale=-0.5)
                y, y_b = yp.next()
                for h in range(4):
                    hs = slice(h * 64, (h + 1) * 64)
                    P.ts(y[:, hs], ops_[:, hs], st8[:, h:h + 1], st8[:, 12 + h:13 + h], ALU.subtract, ALU.mult, [ops_b, st8_b], [y_b])
                P.tt(y[:, :], y[:, :], gnb[:, :], ALU.mult, [y_b, gnb_b], [y_b])
                oc, oc_b = ocp.next()
                P.tt(oc[:, :], y[:, :], sg[:, t, :], ALU.mult, [y_b, sg_b], [oc_b])
                pT, pT_b = P.pb.next()
                for j in range(2):
                    P.tr(pT[:, j * 128:(j + 1) * 128], oc[:, j * 128:(j + 1) * 128], ident[:, :], [oc_b] + IDR, [pT_b])
                P.cp(oT[:, :, tc], pT[:, 0:256].rearrange("p (c k) -> p c k", c=2), [pT_b], [oT_b], eng="act")
            with P.group():
                for j in range(2):
                    P.dma(brT[4 + j, :, gs], oT[:, j, :], [oT_b], [], f"sto{g % 2}", q="pool")

    def phase_C(l, x_src, x_dst):
        P.phase_begin()
        wg, wg_b = P.alloc("wg", [128, 8, 4096], BF16)
        wb, wb_b = P.alloc("wb", [128, 8, D], BF16)
        wo, wo_b = P.alloc("wo", [128, 8, D], BF16)
        for c in range(8):
            P.dma(wg[:, c, :], w_gate[l, c * 128:(c + 1) * 128, :], [], [wg_b], f"w{c % 4}", q="pool")
        P.dma(wb[:, :, :], w_branch[l].rearrange("n (c p) d -> p (n c) d", p=128), [], [wb_b], "w0", q="pool")
        P.dma(wo[:, :, :], w_out[l].rearrange("(c p) d -> p c d", p=128), [], [wo_b], "w1", q="pool")
        gpo, gpo_b = P.alloc("gpo", [128, D], F32)
        P.dma(gpo[:, :], g_post_mix[l:l + 1, :].to_broadcast([128, D]), [], [gpo_b], "cst")
        hTp = P.pool("hT", [128, 8, 512], BF16, 2)
        brp = P.pool("br", [128, 8, 512], BF16, 2)
        sgp = P.pool("sg", [128, 512], F32, 3)
        mgp = P.pool("mg", [128, 512], F32, 2)
        tmp = P.pool("tmp", [128, 512], F32, 2)
        mTp = P.pool("mT", [128, 8, 512], BF16, 2)
        xpool = P.pool("xt", [128, D], F32, 2)
        ypool = P.pool("yt", [128, D], F32, 2)
        small = P.pool("sm", [128, 8], F32, 4)
        junk, junk_b = P.alloc("junk", [128, 512], BF16)
        for g in range(NG):
            gs = slice(g * 512, (g + 1) * 512)
            hTg, hTg_b = hTp.next()
            brg, brg_b = brp.next()
            P.dma(hTg[:, :, :], hTd[:, :, gs].rearrange("c p s -> p c s"), [], [hTg_b], f"lh{g % 2}")
            P.dma(brg[:, :, :], brT[:, :, gs].rearrange("c p s -> p c s"), [], [brg_b], f"lb{g % 2}")
            mT, mT_b = mTp.next()
            for dc in range(8):
                ds_ = slice(dc * 128, (dc + 1) * 128)
                mg, mg_b = mgp.next()
                for n in range(4):
                    psg, psg_b = P.pf.next()
                    for c in range(8):
                        P.mm(psg[:, :], wg[:, c, n * D + dc * 128:n * D + (dc + 1) * 128], hTg[:, c, :], c == 0, c == 7, [wg_b, hTg_b], [psg_b])
                    psp, psp_b = P.pf.next()
                    for c2 in range(2):
                        P.mm(psp[:, :], wb[:, n * 2 + c2, ds_], brg[:, n * 2 + c2, :], c2 == 0, c2 == 1, [wb_b, brg_b], [psp_b])
                    sg, sg_b = sgp.next()
                    P.act(sg[:, :], psg[:, :], AF.Sigmoid, [psg_b], [sg_b])
                    if n == 0:
                        P.tt(mg[:, :], sg[:, :], psp[:, :], ALU.mult, [sg_b, psp_b], [mg_b])
                    else:
                        tm, tm_b = tmp.next()
                        P.tt(tm[:, :], sg[:, :], psp[:, :], ALU.mult, [sg_b, psp_b], [tm_b])
                        if n < 3:
                            P.tt(mg[:, :], mg[:, :], tm[:, :], ALU.add, [mg_b, tm_b], [mg_b], eng="pool")
                        else:
                            P.tt(mT[:, dc, :], mg[:, :], tm[:, :], ALU.add, [mg_b, tm_b], [mT_b], eng="pool")
            for t in range(4):
                tok0 = g * 512 + t * 128
                tc = slice(t * 128, (t + 1) * 128)
                xt, xt_b = xpool.next()
                P.dma(xt[:, :], x_src[tok0:tok0 + 128, :], [], [xt_b], f"x{xpool.i % 2}")
                pss = []
                for half in range(2):
                    pso, pso_b = P.pf.next()
                    for dc in range(8):
                        P.mm(pso[:, :], mT[:, dc, tc], wo[:, dc, half * 512:(half + 1) * 512], dc == 0, dc == 7, [mT_b, wo_b], [pso_b])
                    pss.append((pso, pso_b))
                post_norm(pss, gpo, gpo_b, xt, xt_b, small, junk, junk_b, ypool, x_dst, tok0)

    def post_norm(pss, gpo, gpo_b, xt, xt_b, small, junk, junk_b, ypool, x_dst, tok0):
        ss, ss_b = small.next()
        for half in range(2):
            P.act(junk[:, 0:512], pss[half][0][:, :], AF.Square, [pss[half][1]], [junk_b, ss_b], accum=ss[:, half:half + 1])
        P.tt(ss[:, 2:3], ss[:, 0:1], ss[:, 1:2], ALU.add, [ss_b], [ss_b])
        P.act(ss[:, 3:4], ss[:, 2:3], AF.Ln, [ss_b], [ss_b], scale=1.0 / D, bias=EPS)
        P.act(ss[:, 4:5], ss[:, 3:4], AF.Exp, [ss_b], [ss_b], scale=-0.5)
        yt, yt_b = ypool.next()
        for half in range(2):
            hs = slice(half * 512, (half + 1) * 512)
            P.stt(yt[:, hs], pss[half][0][:, :], ss[:, 4:5], gpo[:, hs], ALU.mult, ALU.mult, [pss[half][1], ss_b, gpo_b], [yt_b])
        P.tt(yt[:, :], yt[:, :], xt[:, :], ALU.add, [yt_b, xt_b], [yt_b], eng="pool")
        P.dma(x_dst[tok0:tok0 + 128, :], yt[:, :], [yt_b], [], P.skey(), q="pool")

    def phase_D(l, x_src, x_dst):
        P.phase_begin()
        wg, wg_b = P.alloc("wfg", [128, 8, DFF], BF16)
        wu, wu_b = P.alloc("wfu", [128, 8, DFF], BF16)
        wd, wd_b = P.alloc("wfd", [128, FC, D], BF16)
        for c in range(8):
            P.dma(wg[:, c, :], w_ffn_gate[l, c * 128:(c + 1) * 128, :], [], [wg_b], f"w{c % 4}", q="pool")
            P.dma(wu[:, c, :], w_ffn_up[l, c * 128:(c + 1) * 128, :], [], [wu_b], f"w{c % 4}", q="pool")
        for f0 in range(0, FC, 6):
            f1 = min(FC, f0 + 6)
            P.dma(wd[:, f0:f1, :], w_ffn_down[l, f0 * 128:f1 * 128, :].rearrange("(c p) d -> p c d", p=128), [], [wd_b], f"w{(f0 // 6) % 4}", q="pool")
        gpo, gpo_b = P.alloc("gpo", [128, D], F32)
        P.dma(gpo[:, :], g_post_ffn[l:l + 1, :].to_broadcast([128, D]), [], [gpo_b], "cst")
        xpool = P.pool("xt", [128, D], F32, 2)
        junk, junk_b = P.alloc("junk", [128, D], BF16)
        hn1, hn1_b = P.alloc("hn", [128, D], BF16)
        hT, hT_b = P.alloc("hT", [128, 8, 512], BF16)
        aT, aT_b = P.alloc("aT", [128, FC, 512], BF16)
        sgp = P.pool("sg", [128, 512], F32, 2)
        ypool = P.pool("yt", [128, D], F32, 2)
        small = P.pool("sm", [128, 8], F32, 4)
        gcol = (l * 2 + 1) * 8
        for g in range(NG):
            for t in range(4):
                tok0 = g * 512 + t * 128
                xt, xt_b = xpool.next()
                P.dma(xt[:, :], x_src[tok0:tok0 + 128, :], [], [xt_b], f"x{xpool.i % 2}")
                ss, ss_b = small.next()
                P.act(junk[:, :], xt[:, :], AF.Square, [xt_b], [junk_b, ss_b], accum=ss[:, 0:1])
                P.act(ss[:, 1:2], ss[:, 0:1], AF.Ln, [ss_b], [ss_b], scale=1.0 / D, bias=EPS)
                P.act(ss[:, 2:3], ss[:, 1:2], AF.Exp, [ss_b], [ss_b], scale=-0.5)
                P.act(hn1[:, :], xt[:, :], AF.Copy, [xt_b, ss_b], [hn1_b], scale=ss[:, 2:3])
                pT, pT_b = P.pb.next()
                for c in range(8):
                    P.tr(pT[:, c * 128:(c + 1) * 128], hn1[:, c * 128:(c + 1) * 128], ident[:, :], [hn1_b] + IDR, [pT_b])
                P.tt(hT[:, :, t * 128:(t + 1) * 128], pT[:, :].rearrange("p (c k) -> p c k", c=8),
                     gpre[:, gcol:gcol + 8].unsqueeze(2).to_broadcast([128, 8, 128]), ALU.mult, [pT_b, gpre_b], [hT_b])
            for fc in range(FC):
                fs = slice(fc * 128, (fc + 1) * 128)
                psg, psg_b = P.pf.next()
                for c in range(8):
                    P.mm(psg[:, :], wg[:, c, fs], hT[:, c, :], c == 0, c == 7, [wg_b, hT_b], [psg_b])
                psu, psu_b = P.pf.next()
                for c in range(8):
                    P.mm(psu[:, :], wu[:, c, fs], hT[:, c, :], c == 0, c == 7, [wu_b, hT_b], [psu_b])
                sg, sg_b = sgp.next()
                P.act(sg[:, :], psg[:, :], AF.Silu, [psg_b], [sg_b])
                P.tt(aT[:, fc, :], sg[:, :], psu[:, :], ALU.mult, [sg_b, psu_b], [aT_b])
            for t in range(4):
                tok0 = g * 512 + t * 128
                tc = slice(t * 128, (t + 1) * 128)
                xt, xt_b = xpool.next()
                P.dma(xt[:, :], x_src[tok0:tok0 + 128, :], [], [xt_b], f"x{xpool.i % 2}")
                pss = []
                for half in range(2):
                    pso, pso_b = P.pf.next()
                    for fc in range(FC):
                        P.mm(pso[:, :], aT[:, fc, tc], wd[:, fc, half * 512:(half + 1) * 512], fc == 0, fc == FC - 1, [aT_b, wd_b], [pso_b])
                    pss.append((pso, pso_b))
                post_norm(pss, gpo, gpo_b, xt, xt_b, small, junk, junk_b, ypool, x_dst, tok0)

    cur = x_in
    for l in range(L):
        import os as _os
        ph = _os.environ.get("KPH", "A1234CD")
        if "A" in ph:
            phase_A(l, cur)
        if stop_after == "A":
            break
        if "1" in ph:
            attn_full(fqk, fv, 70, None, 0)
        if "2" in ph:
            attn_full(mqk, mv, 96, float(96 ** -0.5), 3)
        if "3" in ph:
            attn_dil()
        if "4" in ph:
            ret_phase(l)
        if stop_after == "B":
            break
        phase_C(l, cur, xa)
        if stop_after == "C":
            break
        dst = out if l == L - 1 else xb
        phase_D(l, xa, dst)
        cur = xb
    P.emit()
    return nc, P


_CACHE = {}


def kernel(**inputs):
    S = inputs["x"].shape[1]
    B = inputs["x"].shape[0]
    DEPTH = inputs["w_in"].shape[0]
    key = (S, DEPTH)
    if key not in _CACHE:
        _CACHE[key] = (build(S, DEPTH)[0], host_consts(S))
    nc, consts = _CACHE[key]
    shared = {k: np.ascontiguousarray(np.asarray(v, dtype=np.float32)) for k, v in inputs.items() if k != "x"}
    shared.update(consts)
    x = np.asarray(inputs["x"], dtype=np.float32)
    in_maps = []
    for b in range(B):
        m = dict(shared)
        m["x"] = np.ascontiguousarray(x[b])
        in_maps.append(m)
    res = run_bass_kernel_spmd(nc, in_maps, core_ids=list(range(B)))
    return np.stack([np.asarray(r["out"]) for r in res.results], axis=0).astype(np.float32)
```
